# Optimizing a Trainium2 kernel written in Bass

```python
import math
import jax, jax.numpy as jnp
from jax import lax
import numpy as np

D_MODEL = 1024
BATCH = 32
SEQ = 2048
DEPTH = 4

MEM_LEN = 256
ROPE_THETA = 10000.0
EPS = 1e-6
Q_BLOCK = 128
MAX_POS_OFFSET = 4096

DIFF_HEADS = 8
DIFF_HEAD_DIM = 64
DIFF_V_DIM = 2 * DIFF_HEAD_DIM
DIFF_QK_W = DIFF_HEADS * 2 * DIFF_HEAD_DIM
DIFF_V_W = DIFF_HEADS * DIFF_V_DIM

MLA_HEADS = 8
MLA_Q_LORA = 384
MLA_KV_LORA = 256
MLA_NOPE_DIM = 64
MLA_ROPE_DIM = 32
MLA_V_DIM = 64
MLA_QK_DIM = MLA_NOPE_DIM + MLA_ROPE_DIM
MLA_V_W = MLA_HEADS * MLA_V_DIM

CROSS_HEADS = 4
CROSS_HEAD_DIM = 128
CROSS_W = CROSS_HEADS * CROSS_HEAD_DIM

N_BRANCHES = 3
D_FF = 4 * D_MODEL

IN_SIZES = (DIFF_QK_W, DIFF_QK_W, DIFF_V_W, MLA_Q_LORA, MLA_KV_LORA + MLA_ROPE_DIM, CROSS_W, N_BRANCHES * D_MODEL)
IN_WIDTH = sum(IN_SIZES)

kernel_name = "hybrid_diffattn_mla_memxattn_sqrelu"


def rmsnorm(x, gain):
    xf = x.astype(jnp.float32)
    y = xf * lax.rsqrt(jnp.mean(xf * xf, axis=-1, keepdims=True) + EPS)
    return (y * gain.astype(jnp.float32)).astype(x.dtype)


def rotary_tables(positions, dim):
    inv_freq = ROPE_THETA ** (-jnp.arange(0, dim, 2, dtype=jnp.float32) / dim)
    ang = positions.astype(jnp.float32)[..., None] * inv_freq
    return jnp.cos(ang), jnp.sin(ang)


def rotary(x, cos, sin):
    xf = x.astype(jnp.float32)
    x1, x2 = jnp.split(xf, 2, axis=-1)
    return jnp.concatenate([x1 * cos - x2 * sin, x2 * cos + x1 * sin], axis=-1).astype(x.dtype)


def causal_block_attention(q, k, v, scale):
    B, S, H, M, Dk = q.shape
    nb = S // Q_BLOCK
    qb = q.reshape(B, nb, Q_BLOCK, H, M, Dk).transpose(1, 0, 2, 3, 4, 5)
    key_pos = jnp.arange(S)

    def one_block(args):
        q_blk, blk = args
        s = jnp.einsum('bqhmd,bkhmd->bhmqk', q_blk, k).astype(jnp.float32) * scale
        q_pos = blk * Q_BLOCK + jnp.arange(Q_BLOCK)
        mask = key_pos[None, :] <= q_pos[:, None]
        s = jnp.where(mask, s, -jnp.inf)
        p = jax.nn.softmax(s, axis=-1).astype(v.dtype)
        return jnp.einsum('bhmqk,bkhd->bqhmd', p, v)

    out = lax.map(one_block, (qb, jnp.arange(nb)))
    return out.transpose(1, 0, 2, 3, 4, 5).reshape(B, S, H, M, v.shape[-1])


def diff_attention(q, k, v, q_gain, k_gain, lam_vecs, out_gain, cos, sin, layer_idx):
    B, S, _ = q.shape
    q = q.reshape(B, S, DIFF_HEADS, 2, DIFF_HEAD_DIM)
    k = k.reshape(B, S, DIFF_HEADS, 2, DIFF_HEAD_DIM)
    v = v.reshape(B, S, DIFF_HEADS, DIFF_V_DIM)
    c, s_ = cos[:, :, None, None, :], sin[:, :, None, None, :]
    q = rotary(rmsnorm(q, q_gain), c, s_)
    k = rotary(rmsnorm(k, k_gain), c, s_)
    o = causal_block_attention(q, k, v, DIFF_HEAD_DIM ** -0.5)
    lam_init = 0.8 - 0.6 * math.exp(-0.3 * layer_idx)
    lv = lam_vecs.astype(jnp.float32)
    lam = jnp.exp(jnp.sum(lv[0] * lv[1])) - jnp.exp(jnp.sum(lv[2] * lv[3])) + lam_init
    o = o[:, :, :, 0, :] - lam.astype(o.dtype) * o[:, :, :, 1, :]
    o = rmsnorm(o, out_gain) * (1.0 - lam_init)
    return o.reshape(B, S, DIFF_V_W)


def latent_attention(c_q, c_kv_kr, qa_gain, w_qb, kva_gain, w_kvb, q_gain, k_gain, cos, sin):
    B, S, _ = c_q.shape
    q = (rmsnorm(c_q, qa_gain) @ w_qb).reshape(B, S, MLA_HEADS, MLA_QK_DIM)
    q_nope, q_rope = jnp.split(q, [MLA_NOPE_DIM], axis=-1)
    q_rope = rotary(q_rope, cos[:, :, None, :], sin[:, :, None, :])
    c_kv, k_rope = jnp.split(c_kv_kr, [MLA_KV_LORA], axis=-1)
    kv = (rmsnorm(c_kv, kva_gain) @ w_kvb).reshape(B, S, MLA_HEADS, MLA_NOPE_DIM + MLA_V_DIM)
    k_nope, v = jnp.split(kv, [MLA_NOPE_DIM], axis=-1)
    k_rope = rotary(k_rope, cos, sin)
    k_rope = jnp.broadcast_to(k_rope[:, :, None, :], (B, S, MLA_HEADS, MLA_ROPE_DIM))
    q = rmsnorm(jnp.concatenate([q_nope, q_rope], axis=-1), q_gain)
    k = rmsnorm(jnp.concatenate([k_nope, k_rope], axis=-1), k_gain)
    o = causal_block_attention(q[:, :, :, None, :], k[:, :, :, None, :], v, MLA_QK_DIM ** -0.5)
    return o.reshape(B, S, MLA_V_W)


def memory_cross_attention(q, mem_n, w_mem_kv, q_gain, k_gain):
    B, S, _ = q.shape
    q = rmsnorm(q.reshape(B, S, CROSS_HEADS, CROSS_HEAD_DIM), q_gain)
    kv = (mem_n @ w_mem_kv).reshape(B, mem_n.shape[1], 2, CROSS_HEADS, CROSS_HEAD_DIM)
    k = rmsnorm(kv[:, :, 0], k_gain)
    v = kv[:, :, 1]
    s = jnp.einsum('bshd,bmhd->bhsm', q, k).astype(jnp.float32) * (CROSS_HEAD_DIM ** -0.5)
    p = jax.nn.softmax(s, axis=-1).astype(v.dtype)
    o = jnp.einsum('bhsm,bmhd->bshd', p, v)
    return o.reshape(B, S, CROSS_W)


def setup_inputs(seed: int = 0) -> dict:
    key = jax.random.key(seed)
    ks = iter(jax.random.split(key, 40))
    L = DEPTH

    def nrm(shape, fan_in, scale=1.0):
        return jax.random.normal(next(ks), shape, jnp.float32) * (scale * fan_in ** -0.5)

    def gain(shape):
        return 1.0 + 0.02 * jax.random.normal(next(ks), shape, jnp.float32)

    res_scale = (2 * DEPTH) ** -0.5
    x = jax.random.normal(next(ks), (BATCH, SEQ, D_MODEL), jnp.float32)
    mem = jax.random.normal(next(ks), (BATCH, MEM_LEN, D_MODEL), jnp.float32)
    offset = jax.random.randint(next(ks), (BATCH, 1), 0, MAX_POS_OFFSET, dtype=jnp.int32)
    positions = (offset + jnp.arange(SEQ, dtype=jnp.int32)[None, :]).astype(jnp.int32)
    return {
        "x": x,
        "mem": mem,
        "positions": positions,
        "g_mix": gain((L, D_MODEL)),
        "w_in": nrm((L, D_MODEL, IN_WIDTH), D_MODEL),
        "b_gate": 0.01 * jax.random.normal(next(ks), (L, N_BRANCHES * D_MODEL), jnp.float32),
        "diff_q_gain": gain((L, DIFF_HEAD_DIM)),
        "diff_k_gain": gain((L, DIFF_HEAD_DIM)),
        "diff_lambda": 0.1 * jax.random.normal(next(ks), (L, 4, DIFF_HEAD_DIM), jnp.float32),
        "diff_out_gain": gain((L, DIFF_V_DIM)),
        "w_diff_o": nrm((L, DIFF_V_W, D_MODEL), DIFF_V_W),
        "mla_qa_gain": gain((L, MLA_Q_LORA)),
        "w_mla_qb": nrm((L, MLA_Q_LORA, MLA_HEADS * MLA_QK_DIM), MLA_Q_LORA),
        "mla_kva_gain": gain((L, MLA_KV_LORA)),
        "w_mla_kvb": nrm((L, MLA_KV_LORA, MLA_HEADS * (MLA_NOPE_DIM + MLA_V_DIM)), MLA_KV_LORA),
        "mla_q_gain": gain((L, MLA_QK_DIM)),
        "mla_k_gain": gain((L, MLA_QK_DIM)),
        "w_mla_o": nrm((L, MLA_V_W, D_MODEL), MLA_V_W),
        "mem_gain": gain((L, D_MODEL)),
        "w_mem_kv": nrm((L, D_MODEL, 2 * CROSS_W), D_MODEL),
        "cross_q_gain": gain((L, CROSS_HEAD_DIM)),
        "cross_k_gain": gain((L, CROSS_HEAD_DIM)),
        "w_cross_o": nrm((L, CROSS_W, D_MODEL), CROSS_W),
        "w_out": nrm((L, D_MODEL, D_MODEL), D_MODEL, res_scale),
        "g_mlp": gain((L, D_MODEL)),
        "w_mlp1": nrm((L, D_MODEL, D_FF), D_MODEL),
        "w_mlp2": nrm((L, D_FF, D_MODEL), D_FF, res_scale),
    }


def reference(x, mem, positions, g_mix, w_in, b_gate, diff_q_gain, diff_k_gain, diff_lambda,
              diff_out_gain, w_diff_o, mla_qa_gain, w_mla_qb, mla_kva_gain, w_mla_kvb,
              mla_q_gain, mla_k_gain, w_mla_o, mem_gain, w_mem_kv, cross_q_gain, cross_k_gain,
              w_cross_o, w_out, g_mlp, w_mlp1, w_mlp2):
    B, S, D = x.shape
    cos_d, sin_d = rotary_tables(positions, DIFF_HEAD_DIM)
    cos_r, sin_r = rotary_tables(positions, MLA_ROPE_DIM)
    split_idx = np.cumsum(IN_SIZES)[:-1].tolist()

    for l in range(DEPTH):
        h = rmsnorm(x, g_mix[l])
        dq, dk, dv, c_q, c_kv_kr, xq, gate_logits = jnp.split(h @ w_in[l], split_idx, axis=-1)

        y_diff = diff_attention(dq, dk, dv, diff_q_gain[l], diff_k_gain[l], diff_lambda[l],
                                diff_out_gain[l], cos_d, sin_d, l) @ w_diff_o[l]
        y_mla = latent_attention(c_q, c_kv_kr, mla_qa_gain[l], w_mla_qb[l], mla_kva_gain[l],
                                 w_mla_kvb[l], mla_q_gain[l], mla_k_gain[l], cos_r, sin_r) @ w_mla_o[l]
        mem_n = rmsnorm(mem, mem_gain[l])
        y_mem = memory_cross_attention(xq, mem_n, w_mem_kv[l], cross_q_gain[l], cross_k_gain[l]) @ w_cross_o[l]

        gates = jax.nn.sigmoid((gate_logits + b_gate[l]).reshape(B, S, N_BRANCHES, D))
        merged = gates[:, :, 0] * y_diff + gates[:, :, 1] * y_mla + gates[:, :, 2] * y_mem
        x = x + merged @ w_out[l]

        u = rmsnorm(x, g_mlp[l]) @ w_mlp1[l]
        x = x + jnp.square(jax.nn.relu(u)) @ w_mlp2[l]
    return x
```

```python
import contextlib
import math
import numpy as np
import concourse.bass as bass
import concourse.mybir as mybir
from concourse.bass_utils import run_bass_kernel_spmd

F32 = mybir.dt.float32
BF16 = mybir.dt.bfloat16
I32 = mybir.dt.int32
AF = mybir.ActivationFunctionType
ALU = mybir.AluOpType
AX = mybir.AxisListType

ENGS = ("pe", "act", "dve", "pool", "sp")

D = 1024
SEQ = 2048
NT = 16
DEPTH = 4
EPS = 1e-6
IN_W = 7328
C_DQ, C_DK, C_DV, C_CQ, C_CKV, C_XQ, C_GATE = 0, 1024, 2048, 3072, 3456, 3744, 4256
NEG = -30000.0
SAME_ENGINE_SAFE = 1 << 30


class Buf:
    __slots__ = ("name", "w", "rd")

    def __init__(self, name):
        self.name = name
        self.w = None
        self.rd = {}


class Sched:
    def __init__(self):
        self.ops = {e: [] for e in ENGS}
        self.seen_e = {e: {f: -1 for f in ENGS} for e in ENGS}
        self.seen_d = {e: {} for e in ENGS}
        self.dma_cnt = {}
        self.lastc = {e: -1 for e in ENGS}

    def _filter(self, eng, deps, fsize):
        waits = []
        for tok in deps:
            if tok[0] == "e":
                _, f, i, fs = tok
                if f == eng:
                    if eng in ("pe", "sp"):
                        continue
                    if min(fs, fsize) >= SAME_ENGINE_SAFE:
                        continue
                if self.seen_e[eng][f] >= i:
                    continue
                self.seen_e[eng][f] = i
                waits.append(tok)
            else:
                _, key, val = tok
                if self.seen_d[eng].get(key, 0) >= val:
                    continue
                self.seen_d[eng][key] = val
                waits.append(tok)
        return waits

    @staticmethod
    def _deps(reads, writes):
        deps = []
        for b in reads:
            if b.w is not None:
                deps.append(b.w)
        for b in writes:
            if b.w is not None:
                deps.append(b.w)
            deps.extend(b.rd.values())
        return deps

    def op(self, eng, fn, reads=(), writes=(), fsize=512):
        idx = len(self.ops[eng])
        waits = self._filter(eng, self._deps(reads, writes), fsize)
        self.ops[eng].append((waits, fn, None))
        self.lastc[eng] = idx
        tok = ("e", eng, idx, fsize)
        for b in reads:
            b.rd[eng] = tok
        for b in writes:
            b.w = tok
            b.rd = {}
        return tok

    def dma(self, eng, key, fn, reads=(), writes=(), batch_left=1):
        prev = 16 * self.dma_cnt.get(key, 0)
        deps = [t for t in self._deps(reads, writes) if not (t[0] == "d" and t[1] == key and t[2] > prev)]
        waits = self._filter(eng, deps, 1 << 20)
        cnt = self.dma_cnt.get(key, 0) + 1
        self.dma_cnt[key] = cnt
        self.ops[eng].append((waits, fn, key))
        tok = ("d", key, 16 * (cnt + batch_left - 1))
        for b in reads:
            b.rd[key] = tok
        for b in writes:
            b.w = tok
            b.rd = {}
        return tok

    def barrier(self):
        import os
        if os.environ.get('NOBAR'):
            return
        toks = []
        for e in ENGS:
            if self.lastc[e] >= 0:
                toks.append(("e", e, self.lastc[e], 0))
        for k, c in self.dma_cnt.items():
            toks.append(("d", k, 16 * c))
        for e in ENGS:
            mine = [t for t in toks if not (t[0] == "e" and t[1] == e and e in ("pe", "sp"))]
            waits = self._filter(e, mine, 0)
            if waits:
                self.ops[e].append((waits, None, None))

    def wait_all_dma(self, eng, keys):
        toks = [("d", k, 16 * self.dma_cnt[k]) for k in keys if k in self.dma_cnt]
        waits = self._filter(eng, toks, 0)
        if waits:
            self.ops[eng].append((waits, None, None))

    def prepare(self):
        mil = {e: set() for e in ENGS}
        for e in ENGS:
            for waits, fn, key in self.ops[e]:
                for tok in waits:
                    if tok[0] == "e":
                        mil[tok[1]].add(tok[2])
        self.rank = {}
        for e in ENGS:
            for r, i in enumerate(sorted(mil[e])):
                assert self.ops[e][i][1] is not None and self.ops[e][i][2] is None
                self.rank[(e, i)] = r + 1

    def check(self):
        pc = {e: 0 for e in ENGS}
        sem = {e: 0 for e in ENGS}
        dsem = {}
        progress = True
        while progress:
            progress = False
            for e in ENGS:
                while pc[e] < len(self.ops[e]):
                    waits, fn, key = self.ops[e][pc[e]]
                    ok = True
                    for tok in waits:
                        if tok[0] == "e":
                            if sem[tok[1]] < self.rank[(tok[1], tok[2])]:
                                ok = False
                        elif dsem.get(tok[1], 0) < tok[2]:
                            ok = False
                    if not ok:
                        break
                    if key is not None:
                        dsem[key] = dsem.get(key, 0) + 16
                    elif (e, pc[e]) in self.rank:
                        sem[e] += 1
                        assert sem[e] == self.rank[(e, pc[e])]
                    pc[e] += 1
                    progress = True
        stuck = {e: (pc[e], self.ops[e][pc[e]][0]) for e in ENGS if pc[e] < len(self.ops[e])}
        assert not stuck, ("DEADLOCK", stuck, sem, dsem)

    def emit_one(self, e, eo, sems, dma_sems):
        rank = self.rank
        for i, (waits, fn, key) in enumerate(self.ops[e]):
            for tok in waits:
                if tok[0] == "e":
                    eo.wait_ge(sems[tok[1]], rank[(tok[1], tok[2])])
                else:
                    eo.wait_ge(dma_sems[tok[1]], tok[2])
            if fn is None:
                continue
            ins = fn(eo)
            if key is not None:
                ins.then_inc(dma_sems[key], 16)
            elif (e, i) in rank:
                ins.then_inc(sems[e], 1)


class Arena:
    def __init__(self, t, nbytes):
        self.t = t
        self.n = nbytes
        self.off = 0

    def reset(self, off=0):
        self.off = off

    def at(self, off, shape, dtype):
        esz = 4 if dtype in (F32, I32) else 2
        n = int(np.prod(shape[1:])) * esz
        assert off % 4 == 0 and off + n <= self.n, (off, n, self.n)
        v = self.t[0:shape[0], off // 2:(off + n) // 2]
        if esz == 4:
            v = v.bitcast(dtype)
        if len(shape) == 3:
            v = v.rearrange("p (a b) -> p a b", a=shape[1])
        elif len(shape) == 4:
            v = v.rearrange("p (a b c) -> p a b c", a=shape[1], b=shape[2])
        return v

    def alloc(self, shape, dtype):
        esz = 4 if dtype in (F32, I32) else 2
        n = int(np.prod(shape[1:])) * esz
        n = (n + 3) // 4 * 4
        v = self.at(self.off, shape, dtype)
        self.off += n
        return v


WEIGHT_SPECS = [
    ("w_in", (DEPTH, 1024, IN_W)), ("w_diff_o", (DEPTH, 1024, 1024)), ("w_mla_qb", (DEPTH, 384, 768)),
    ("w_mla_kvb", (DEPTH, 256, 1024)), ("w_mla_o", (DEPTH, 512, 1024)), ("w_mem_kv", (DEPTH, 1024, 1024)),
    ("w_cross_o", (DEPTH, 512, 1024)), ("w_out", (DEPTH, 1024, 1024)), ("w_mlp1", (DEPTH, 1024, 4096)),
    ("w_mlp2", (DEPTH, 4096, 1024)),
]
VEC_SPECS = [
    ("g_mix", (DEPTH, 1024)), ("b_gate", (DEPTH, 3072)), ("diff_q_gain", (DEPTH, 64)), ("diff_k_gain", (DEPTH, 64)),
    ("diff_lambda", (DEPTH, 4, 64)), ("diff_out_gain", (DEPTH, 128)), ("mla_qa_gain", (DEPTH, 384)),
    ("mla_kva_gain", (DEPTH, 256)), ("mla_q_gain", (DEPTH, 96)), ("mla_k_gain", (DEPTH, 96)),
    ("mem_gain", (DEPTH, 1024)), ("cross_q_gain", (DEPTH, 128)), ("cross_k_gain", (DEPTH, 128)),
    ("g_mlp", (DEPTH, 1024)),
]
GO = {}
_o = 0
for _n, _w in [("dqg", 64), ("dkg", 64), ("dog", 128), ("qag", 384), ("kvag", 256), ("mqg", 96), ("mkg", 96),
               ("cqg", 128), ("ckg", 128), ("lam", 256), ("mqk", 96), ("cqk", 128), ("dog2", 128), ("lamv", 8)]:
    GO[_n] = (_o, _w)
    _o += _w
GL_W = _o


def build_program(n_seq=4, depth=DEPTH, phases="ABCDE"):
    nc = bass.Bass("TRN2", target_bir_lowering=False)
    dd = {}
    dd["x"] = nc.dram_tensor("x", [n_seq, SEQ, D], F32, kind="ExternalInput").ap()
    dd["mem"] = nc.dram_tensor("mem", [n_seq, 256, D], F32, kind="ExternalInput").ap()
    dd["pos"] = nc.dram_tensor("pos", [n_seq, 128, NT], I32, kind="ExternalInput").ap()
    dd["consts"] = nc.dram_tensor("consts", [128, 512], F32, kind="ExternalInput").ap()
    for n, shp in WEIGHT_SPECS + VEC_SPECS:
        dd[n] = nc.dram_tensor(n, list(shp), F32, kind="ExternalInput").ap()
    out_d = nc.dram_tensor("out", [n_seq, SEQ, D], F32, kind="ExternalOutput").ap()

    es = contextlib.ExitStack()
    with es:
        def sb(name, shape, dtype):
            return es.enter_context(nc.sbuf_tensor(name, shape, dtype))

        xs = sb("xs", [128, NT, D], F32)
        ident = sb("ident", [128, 128], BF16)
        maskneg = sb("maskneg", [128, 128], BF16)
        ones_r = sb("ones_r", [1, 128], BF16)
        invf = sb("invf", [128, 48], F32)
        brow = sb("brow", [1, 512], F32)
        gl = sb("gl", [128, GL_W], F32)
        gbig = sb("gbig", [128, D], F32)
        cosd = sb("cosd", [128, NT, 32], F32)
        sind = sb("sind", [128, NT, 32], F32)
        cosr = sb("cosr", [128, NT, 16], F32)
        sinr = sb("sinr", [128, NT, 16], F32)
        NSTAT = 8
        stats = [sb(f"stat{i}", [128, 16], F32) for i in range(NSTAT)]
        NSLOT = 3
        ring = [sb(f"ring{i}", [128, 9, 512], BF16) for i in range(NSLOT)]
        import os
        AR_BYTES = nc.sbuf_bytes_remaining - 512 - int(os.environ.get('ARSHRINK', '0'))
        AR_BYTES -= AR_BYTES % 4
        ar_t = sb("arena", [128, AR_BYTES // 2], BF16)
        ar = Arena(ar_t, AR_BYTES)
        banks = [es.enter_context(nc.psum_tensor(f"pb{i}", [128, 512], F32)) for i in range(8)]

        sems = {e: es.enter_context(nc.semaphore("s_" + e)) for e in ENGS}
        dkeys = ["cst2", "bias", "w0", "w1", "w2", "xl0", "xl1", "xl2", "xl3", "xs0", "xs1", "xs2", "xs3", "cst", "gn", "gb", "pos", "mem"]
        dsems = {k: es.enter_context(nc.semaphore("d_" + k)) for k in dkeys}

        OFF_A = 0
        OFF_B = 32768
        OFF_C = OFF_B + 16384
        OFF_D = OFF_C + 16512
        OFF_W = OFF_D + 8448
        assert OFF_W + 23000 <= AR_BYTES or os.environ.get('ARSHRINK'), (OFF_W, AR_BYTES)

        def program(S, wplan):
            bk = [Buf(f"bank{i}") for i in range(8)]
            bkT = [Buf("T0"), Buf("T1")]
            rot = {"S": 0, "O": 0, "T": 0, "stat": 0}
            B_xs = [Buf(f"xs{t}") for t in range(NT)]
            B_const = Buf("const")
            B_gl = Buf("gl")
            B_gbig = Buf("gbig")
            B_rot = Buf("rot")
            B_stat = [Buf(f"stat{i}") for i in range(NSTAT)]
            B_ring = [Buf(f"ring{i}") for i in range(NSLOT)]
            B_ringb = [Buf(f"ringb{i}") for i in range(NSLOT)]
            B_brow = Buf("brow")
            B_ones = Buf("ones")
            B_invf = Buf("invf")

            def s_bank():
                i = rot["S"] % 3
                rot["S"] += 1
                return banks[i], bk[i]

            def t_half():
                i = rot["T"] % 2
                rot["T"] += 1
                return banks[7][:, i * 256:(i + 1) * 256].bitcast(BF16), bkT[0]

            def stat():
                i = rot["stat"] % NSTAT
                rot["stat"] += 1
                return stats[i], B_stat[i]

            def fsz(ap):
                return int(np.prod(ap.shape[1:]))

            def mm(out, lhsT, rhs, start, stop, rd, wr, skip=False):
                if skip:
                    S.op("pe", lambda e: e.matmul(out, lhsT, rhs, start=start, stop=stop, skip_group_check=True), rd, wr)
                else:
                    S.op("pe", lambda e: e.matmul(out, lhsT, rhs, start=start, stop=stop), rd, wr)

            def tr(out, in_, rd, wr):
                np_ = in_.shape[0]
                S.op("pe", lambda e: e.transpose(out=out, in_=in_, identity=ident[0:np_, 0:np_]), list(rd) + [B_const], wr)

            def act(out, in_, func, rd, wr, scale=1.0, bias=0.0, accum=None):
                f = 1 if accum is not None else fsz(out)
                if accum is not None:
                    S.op("act", lambda e: e.activation(out=out, in_=in_, func=func, bias=bias, scale=scale, accum_out=accum), rd, wr, f)
                else:
                    S.op("act", lambda e: e.activation(out=out, in_=in_, func=func, bias=bias, scale=scale), rd, wr, f)

            def v_tt(out, in0, in1, op, rd, wr, eng="dve"):
                S.op(eng, lambda e: e.tensor_tensor(out=out, in0=in0, in1=in1, op=op), rd, wr, fsz(out))

            def v_ts(out, in0, s1, s2, op0, op1, rd, wr, eng="dve"):
                if op1 is None:
                    S.op(eng, lambda e: e.tensor_scalar(out=out, in0=in0, scalar1=s1, scalar2=None, op0=op0), rd, wr, fsz(out))
                else:
                    S.op(eng, lambda e: e.tensor_scalar(out=out, in0=in0, scalar1=s1, scalar2=s2, op0=op0, op1=op1), rd, wr, fsz(out))

            def v_stt(out, in0, scalar, in1, op0, op1, rd, wr, eng="dve"):
                S.op(eng, lambda e: e.scalar_tensor_tensor(out=out, in0=in0, scalar=scalar, in1=in1, op0=op0, op1=op1), rd, wr, fsz(out))

            def v_copy(out, in_, rd, wr, eng="dve"):
                S.op(eng, lambda e: e.tensor_copy(out=out, in_=in_), rd, wr, fsz(out))

            def v_red(out, in_, rd, wr):
                S.op("dve", lambda e: e.reduce_sum(out=out, in_=in_, axis=AX.X), rd, wr, fsz(out))

            def v_recip(out, in_, rd, wr):
                S.op("dve", lambda e: e.reciprocal(out=out, in_=in_), rd, wr, fsz(out))

            def v_memset(ap, val, wr, eng="dve"):
                S.op(eng, lambda e: e.memset(ap, val), (), wr, fsz(ap))

            def rstd_from_ss(st, stb, G, W):
                act(st[:, 0:G], st[:, 0:G], AF.Ln, [stb], [stb], scale=1.0 / W, bias=EPS)
                act(st[:, 0:G], st[:, 0:G], AF.Exp, [stb], [stb], scale=-0.5)

            wrec = []
            wstate = {"i": 0, "issued": 0}

            def w_issue(k):
                name, l, r0, nr, c0, ncol, bias = wplan[k]
                slot = k % NSLOT
                kc = nr // 128
                src = dd[name][l, r0:r0 + nr, c0:c0 + ncol].rearrange("(c p) n -> p c n", p=128)
                dst = ring[slot][:, 0:kc, 0:ncol]
                key = f"w{slot}"
                S.dma("pool", key, lambda e: e.dma_start(out=dst, in_=src), (), [B_ring[slot]], batch_left=1)
                if bias:
                    bsrc = dd["b_gate"][l:l + 1, bias[0]:bias[0] + ncol]
                    S.dma("sp", "bias", lambda e: e.dma_start(out=brow[0:1, 0:ncol], in_=bsrc), (), [B_brow], batch_left=1)

            def wget(name, l, r0, nr, c0, ncol, bias=None, hold=0):
                desc = (name, l, r0, nr, c0, ncol, bias)
                i = wstate["i"]
                wstate["i"] += 1
                if wplan is None:
                    wrec.append(desc)
                    if bias:
                        return ring[i % NSLOT], [B_ring[i % NSLOT], B_ringb[i % NSLOT]]
                    return ring[i % NSLOT], B_ring[i % NSLOT]
                assert wplan[i] == desc, (i, wplan[i], desc)
                while wstate["issued"] < min(len(wplan), i + NSLOT - hold):
                    w_issue(wstate["issued"])
                    wstate["issued"] += 1
                if bias:
                    v_copy(ring[i % NSLOT][0:1, 8, 0:ncol], brow[0:1, 0:ncol], [B_brow], [B_ringb[i % NSLOT]])
                    return ring[i % NSLOT], [B_ring[i % NSLOT], B_ringb[i % NSLOT]]
                return ring[i % NSLOT], B_ring[i % NSLOT]

            cd = dd["consts"]
            SKIPC = os.environ.get('SKIPC', '')
            if "i" not in SKIPC:
                S.dma("pool", "cst", lambda e: e.dma_start(out=ident[:], in_=cd[:, 0:128]), (), [B_const], batch_left=1)
            if "m" not in SKIPC:
                S.dma("pool", "cst", lambda e: e.dma_start(out=maskneg[:], in_=cd[:, 128:256]), (), [B_const], batch_left=1)
            v_memset(ones_r[:], 1.0, [B_ones])
            if "f" not in SKIPC:
                S.dma("sp", "cst2", lambda e: e.dma_start(out=invf[:], in_=cd[:, 384:432]), (), [B_invf], batch_left=1)

            def load_gains(l):
                lst = [("dqg", "diff_q_gain"), ("dkg", "diff_k_gain"), ("dog", "diff_out_gain"), ("qag", "mla_qa_gain"),
                       ("kvag", "mla_kva_gain"), ("mqg", "mla_q_gain"), ("mkg", "mla_k_gain"), ("cqg", "cross_q_gain"),
                       ("ckg", "cross_k_gain")]
                GN = os.environ.get("GN")
                if GN:
                    lst = lst[int(GN.split(",")[0]):int(GN.split(",")[1])]
                n = len(lst) + (0 if "l" in os.environ.get("GSKIP", "") else 1)
                for i, (gn, dn) in enumerate(lst):
                    o, w = GO[gn]
                    S.dma("sp", "gn", (lambda o, w, dn: lambda e: e.dma_start(out=gl[:, o:o + w], in_=dd[dn][l].partition_broadcast(128)))(o, w, dn),
                          (), [B_gl], batch_left=n - i)
                o, w = GO["lam"]
                GSK = os.environ.get("GSKIP", "")
                if "l" not in GSK:
                  S.dma("sp", "gn", lambda e: e.dma_start(out=gl[:, o:o + w], in_=dd["diff_lambda"][l].rearrange("a b -> (a b)").partition_broadcast(128)),
                      (), [B_gl], batch_left=1)
                if "c" in GSK:
                    return
                g = lambda n_: gl[:, GO[n_][0]:GO[n_][0] + GO[n_][1]]
                v_tt(g("mqk"), g("mqg"), g("mkg"), ALU.mult, [B_gl], [B_gl])
                v_tt(g("cqk"), g("cqg"), g("ckg"), ALU.mult, [B_gl], [B_gl])
                lam_init = 0.8 - 0.6 * math.exp(-0.3 * l)
                v_ts(g("dog2"), g("dog"), 1.0 - lam_init, None, ALU.mult, None, [B_gl], [B_gl])
                lo = GO["lam"][0]
                lv = GO["lamv"][0]
                lam4 = gl[:, lo:lo + 256].rearrange("p (a b) -> p a b", a=4)
                v_tt(gl[:, lo:lo + 64], lam4[:, 0, :], lam4[:, 1, :], ALU.mult, [B_gl], [B_gl])
                v_tt(gl[:, lo + 128:lo + 192], lam4[:, 2, :], lam4[:, 3, :], ALU.mult, [B_gl], [B_gl])
                v_red(gl[:, lv:lv + 1], gl[:, lo:lo + 64], [B_gl], [B_gl])
                v_red(gl[:, lv + 1:lv + 2], gl[:, lo + 128:lo + 192], [B_gl], [B_gl])
                act(gl[:, lv:lv + 2], gl[:, lv:lv + 2], AF.Exp, [B_gl], [B_gl])
                v_stt(gl[:, lv + 2:lv + 3], gl[:, lv + 1:lv + 2], -lam_init, gl[:, lv:lv + 1], ALU.add, ALU.subtract, [B_gl], [B_gl])

            def load_gbig(name, l):
                S.dma("sp", "gb", lambda e: e.dma_start(out=gbig[:], in_=dd[name][l].partition_broadcast(128)), (), [B_gbig])

            def rot_tables(b):
                ar.reset(OFF_W)
                pi_ = ar.alloc([128, NT], I32)
                pf = ar.alloc([128, NT], F32)
                ang = ar.alloc([128, NT, 32], F32)
                ki = ar.alloc([128, NT, 32], I32)
                kf = ar.alloc([128, NT, 32], F32)
                r = ar.alloc([128, NT, 32], F32)
                m = ar.alloc([128, NT, 32], F32)
                Bt = Buf("rt")
                S.dma("sp", "pos", lambda e: e.dma_start(out=pi_, in_=dd["pos"][b]), (), [Bt])
                v_copy(pf, pi_, [Bt], [Bt])
                for (W, o, ct, st_) in ((32, 0, cosd, sind), (16, 32, cosr, sinr)):
                    a_ = ang[:, :, 0:W]
                    v_tt(a_, pf.unsqueeze(2).to_broadcast([128, NT, W]), invf[:, o:o + W].unsqueeze(1).to_broadcast([128, NT, W]), ALU.mult, [Bt, B_invf], [Bt])
                    v_ts(ki[:, :, 0:W], a_, 1.0 / (2 * math.pi), None, ALU.mult, None, [Bt], [Bt])
                    v_copy(kf[:, :, 0:W], ki[:, :, 0:W], [Bt], [Bt])
                    v_stt(r[:, :, 0:W], kf[:, :, 0:W], -6.28125, a_, ALU.mult, ALU.add, [Bt], [Bt])
                    v_stt(r[:, :, 0:W], kf[:, :, 0:W], -0.0019353071795864769, r[:, :, 0:W], ALU.mult, ALU.add, [Bt], [Bt])
                    v_ts(r[:, :, 0:W], r[:, :, 0:W], math.pi, -math.pi, ALU.min, ALU.max, [Bt], [Bt])
                    act(st_[:], r[:, :, 0:W], AF.Sin, [Bt], [B_rot])
                    v_ts(m[:, :, 0:W], r[:, :, 0:W], math.pi / 2, -2 * math.pi, ALU.is_gt, ALU.mult, [Bt], [Bt])
                    v_stt(m[:, :, 0:W], r[:, :, 0:W], math.pi / 2, m[:, :, 0:W], ALU.add, ALU.add, [Bt], [Bt])
                    v_ts(m[:, :, 0:W], m[:, :, 0:W], math.pi, -math.pi, ALU.min, ALU.max, [Bt], [Bt])
                    act(ct[:], m[:, :, 0:W], AF.Sin, [Bt], [B_rot])

            def make_hT(c, hT, B_hT, hb_tiles, B_hb, sq, B_sq, src_tiles=None, tiles=(0, 1, 2, 3), col0=0):
                for t in tiles:
                    if src_tiles is None:
                        src, sbuf_ = xs[:, 4 * c + t, :], B_xs[4 * c + t]
                    else:
                        src, sbuf_ = src_tiles[t]
                    st, stb = stat()
                    act(sq, src, AF.Square, [sbuf_], [B_sq, stb], accum=st[:, 0:1])
                    rstd_from_ss(st, stb, 1, D)
                    hb, hbb = hb_tiles[t % len(hb_tiles)], B_hb[t % len(hb_tiles)]
                    v_stt(hb, src, st[:, 0:1], gbig[:], ALU.mult, ALU.mult, [sbuf_, stb, B_gbig], [hbb])
                    tps = []
                    for half in range(2):
                        tp, tb = t_half()
                        tp3 = tp.rearrange("p (a b) -> p a b", a=4)
                        tps.append(tp3)
                        for k in range(4):
                            kc = half * 4 + k
                            tr(tp3[:, k, :], hb[:, kc * 128:(kc + 1) * 128], [hbb], [tb])
                    v_copy(hT[:, 0:4, col0 + t * 128:col0 + (t + 1) * 128], tps[0], [tb], [B_hT[t]])
                    act(hT[:, 4:8, col0 + t * 128:col0 + (t + 1) * 128], tps[1], AF.Copy, [tb], [B_hT[t]])

            def proj_tile(hT, B_hT_t, t, wslot, wb, kc_n, ncol, extra_bias=False, col=None):
                pb, pbb = s_bank()
                col = t * 128 if col is None else col
                for kc in range(kc_n):
                    mm(pb[:, 0:ncol], hT[:, kc, col:col + 128], wslot[:, kc, 0:ncol], kc == 0, (kc == kc_n - 1) and not extra_bias,
                       [B_hT_t, wb[0] if extra_bias else wb], [pbb])
                if extra_bias:
                    mm(pb[:, 0:ncol], ones_r[0:1, :], wslot[0:1, 8, 0:ncol], False, True, [B_ones, wb[1]], [pbb])
                return pb, pbb

            def rstd_groups(src3, srcb, G, W, sq, B_sq):
                st, stb = stat()
                sq3 = sq.bitcast(F32)[:, 0:G * W].rearrange("p (a b) -> p a b", a=G)
                act(sq3, src3, AF.Square, [srcb], [B_sq])
                v_red(st[:, 0:G], sq3, [B_sq], [stb])
                rstd_from_ss(st, stb, G, W)
                return st, stb

            def attend(c, nk, causal, kT_fn, qT_fn, V_fn, dv1, scale, O_list, pT, B_pT, rdq, rdk, rdv):
                steps = []
                for j in range(nk):
                    r = j - 4 * c if causal else -1
                    q0 = 128 * r if r >= 0 else 0
                    steps.append((j, r, q0))
                sb_ = {}

                def qk(s):
                    j, r, q0 = steps[s]
                    pb, pbb = s_bank()
                    sb_[s] = (pb, pbb)
                    diag = causal and r >= 0
                    mm(pb[:, q0:512], kT_fn(j), qT_fn(q0), True, not diag, rdq + rdk, [pbb])
                    if diag:
                        mm(pb[:, q0:q0 + 128], ident[:], maskneg[:], False, True, [B_const], [pbb])

                def pv(s):
                    j, r, q0 = steps[s]
                    pb, pbb = sb_.pop(s)
                    p_, pbuf = pT[s % len(pT)], B_pT[s % len(pT)]
                    act(p_[:, q0:512], pb[:, q0:512], AF.Exp, [pbb], [pbuf], scale=scale)
                    for i in range(q0 // 128, 4):
                        last = (4 * c + i) if causal else nk - 1
                        oap, ob, first = O_list[i]
                        mm(oap, p_[:, i * 128:(i + 1) * 128], V_fn(j), (j == 0) and first, j == last, [pbuf] + rdv, [ob], skip=True)

                n = len(steps)
                LOOK = 2
                for s in range(min(LOOK, n)):
                    qk(s)
                for s in range(n):
                    if s + LOOK < n:
                        qk(s + LOOK)
                    pv(s)

            for b in range(n_seq):
                for c in range(4):
                    for t in range(4):
                        gt = 4 * c + t
                        S.dma("sp", f"xl{c}", (lambda gt, b: lambda e: e.dma_start(out=xs[:, gt, :], in_=dd["x"][b, gt * 128:(gt + 1) * 128, :]))(gt, b),
                              (), [B_xs[gt]], batch_left=4 - t)
                if 'r' in phases or 'A' in phases or 'B' in phases:
                    rot_tables(b)
                S.barrier()

                for l in range(depth):
                    if 'g' in phases or any(p in phases for p in 'ABCD'):
                        load_gains(l)
                    lam_o = GO["lamv"][0]
                    neg_lam = gl[:, lam_o + 2:lam_o + 3]
                    gsl = lambda n_: gl[:, GO[n_][0]:GO[n_][0] + GO[n_][1]]

                    oT_d = ar.at(OFF_A, [128, 8, SEQ], BF16)
                    oT_m = ar.at(OFF_B, [128, 4, SEQ], BF16)
                    oT_c = ar.at(OFF_C, [128, 4, SEQ], BF16)
                    B_oTd = [Buf(f"oTd{c}") for c in range(4)]
                    B_oTm = [Buf(f"oTm{c}") for c in range(4)]
                    B_oTc = [Buf(f"oTc{c}") for c in range(4)]

                    if "A" in phases or "B" in phases or "C" in phases or "D" in phases:
                        load_gbig("g_mix", l)

                    for g in range(2) if "A" in phases else ():
                        S.barrier()
                        kT = ar.at(OFF_B, [128, 4, SEQ], BF16)
                        Vd = ar.at(OFF_C, [128, NT, 4, 129], BF16)
                        ar.reset(OFF_D)
                        hT = ar.alloc([128, 8, 512], BF16)
                        qT = ar.alloc([128, 4, 512], BF16)
                        sqf = ar.alloc([128, 512], F32)
                        sq = sqf.bitcast(BF16)
                        hb_t = [ar.alloc([128, 1024], BF16) for _ in range(1)]
                        qn = ar.alloc([128, 512], F32)
                        t1 = ar.alloc([128, 512], F32)
                        qb = [ar.alloc([128, 512], BF16) for _ in range(1)]
                        tabs = ar.alloc([128, 4, 4, 64], F32)
                        pT = [ar.alloc([128, 512], BF16) for _ in range(2)]
                        oe = ar.alloc([128, 2, 2, 128], F32)
                        t2 = oe.rearrange("p a b c -> p (a b c)")
                        oe2 = ar.alloc([128, 2, 128], F32)
                        ob16 = ar.alloc([128, 2, 128], BF16)
                        B_hT = [Buf(f"hT{t}") for t in range(4)]
                        B_qT = [Buf(f"qT{t}") for t in range(4)]
                        B_sq, B_hb = Buf("sq"), [Buf("hb0"), Buf("hb1")]
                        B_qn, B_t1, B_t2, B_qb = Buf("qn"), Buf("t1"), Buf("t2"), [Buf("qb0"), Buf("qb1")]
                        B_tabs = Buf("tabs")
                        B_pT = [Buf(f"pT{i}") for i in range(2)]
                        B_oe, B_oe2, B_ob16 = B_t2, Buf("oe2"), Buf("ob16")
                        B_kT = [Buf(f"kT{t}") for t in range(NT)]
                        B_V = [Buf(f"V{t}") for t in range(NT)]
                        v_memset(Vd[:, :, :, 128:129], 1.0, B_V)
                        for c in range(4):
                            make_hT(c, hT, B_hT, hb_t, B_hb, sq, B_sq)
                            cs = cosd[:, 4 * c:4 * c + 4, :]
                            sn = sind[:, 4 * c:4 * c + 4, :]
                            for qi, gn in ((0, "dqg"), (1, "dkg")):
                                g1 = gsl(gn)[:, 0:32].unsqueeze(1).to_broadcast([128, 4, 32])
                                g2 = gsl(gn)[:, 32:64].unsqueeze(1).to_broadcast([128, 4, 32])
                                CC, SS = tabs[:, 2 * qi], tabs[:, 2 * qi + 1]
                                v_tt(CC[:, :, 0:32], cs, g1, ALU.mult, [B_rot, B_gl], [B_tabs])
                                v_tt(CC[:, :, 32:64], cs, g2, ALU.mult, [B_rot, B_gl], [B_tabs])
                                v_stt(SS[:, :, 0:32], sn, -1.0, g2, ALU.mult, ALU.mult, [B_rot, B_gl], [B_tabs])
                                v_tt(SS[:, :, 32:64], sn, g1, ALU.mult, [B_rot, B_gl], [B_tabs])
                            for kind, col0 in (("q", C_DQ), ("k", C_DK)):
                                ws, wb = wget("w_in", l, 0, 1024, col0 + 512 * g, 512)
                                qi = 0 if kind == "q" else 1
                                for t in range(4):
                                    pb, pbb = proj_tile(hT, B_hT[t], t, ws, wb, 8, 512)
                                    p3 = pb[:, :].rearrange("p (a b) -> p a b", a=8)
                                    st, stb = rstd_groups(p3, pbb, 8, 64, sq, B_sq)
                                    qn3 = qn[:, :].rearrange("p (a b) -> p a b", a=8)
                                    v_tt(qn3, p3, st[:, 0:8].unsqueeze(2).to_broadcast([128, 8, 64]), ALU.mult, [pbb, stb], [B_qn])
                                    CC = tabs[:, 2 * qi, t, :].unsqueeze(1).to_broadcast([128, 8, 64])
                                    SS = tabs[:, 2 * qi + 1, t, :]
                                    t13 = t1[:, :].rearrange("p (a b) -> p a b", a=8)
                                    t23 = t2[:, :].rearrange("p (a b) -> p a b", a=8)
                                    v_tt(t13, qn3, CC, ALU.mult, [B_qn, B_tabs], [B_t1])
                                    v_tt(t23[:, :, 0:32], qn3[:, :, 32:64], SS[:, 0:32].unsqueeze(1).to_broadcast([128, 8, 32]), ALU.mult, [B_qn, B_tabs], [B_t2])
                                    v_tt(t23[:, :, 32:64], qn3[:, :, 0:32], SS[:, 32:64].unsqueeze(1).to_broadcast([128, 8, 32]), ALU.mult, [B_qn, B_tabs], [B_t2])
                                    q16, q16b = qb[0], B_qb[0]
                                    v_tt(q16[:, :], t1[:, :], t2[:, :], ALU.add, [B_t1, B_t2], [q16b])
                                    tp, tb = t_half()
                                    tp3 = tp.rearrange("p (a b) -> p a b", a=4)
                                    for hh in range(4):
                                        tr(tp3[:, hh, :], q16[:, hh * 128:(hh + 1) * 128], [q16b], [tb])
                                    if kind == "q":
                                        v_copy(qT[:, :, t * 128:(t + 1) * 128], tp3, [tb], [B_qT[t]])
                                    else:
                                        gt = 4 * c + t
                                        v_copy(kT[:, :, gt * 128:(gt + 1) * 128], tp3, [tb], [B_kT[gt]])
                            ws, wb = wget("w_in", l, 0, 1024, C_DV + 512 * g, 512)
                            for t in range(4):
                                gt = 4 * c + t
                                pb, pbb = proj_tile(hT, B_hT[t], t, ws, wb, 8, 512)
                                act(Vd[:, gt, :, 0:128], pb[:, :].rearrange("p (a b) -> p a b", a=4), AF.Copy, [pbb], [B_V[gt]])
                            nk = 4 * c + 4
                            for hh in range(4):
                                h = 4 * g + hh
                                Oacc = {}
                                for m in range(2):
                                    O_list = []
                                    for i in range(4):
                                        bi = 3 + 2 * m + i // 2
                                        O_list.append((banks[bi][:, (i % 2) * 256:(i % 2) * 256 + 129], bk[bi], i % 2 == 0))
                                    Oacc[m] = O_list
                                    p0 = 64 * m
                                    attend(c, nk, True,
                                           lambda j, p0=p0, hh=hh: kT[p0:p0 + 64, hh, j * 128:(j + 1) * 128],
                                           lambda q0, p0=p0, hh=hh: qT[p0:p0 + 64, hh, q0:512],
                                           lambda j, hh=hh: Vd[:, j, hh, :],
                                           129, 0.125, O_list, pT, B_pT, B_qT, B_kT[0:nk], B_V[0:nk])
                                for half in range(2):
                                    b1, b2 = banks[3 + half], banks[5 + half]
                                    bb1, bb2 = bk[3 + half], bk[5 + half]
                                    v1 = b1[:, :].rearrange("p (a b) -> p a b", a=2)
                                    v2 = b2[:, :].rearrange("p (a b) -> p a b", a=2)
                                    st, stb = stat()
                                    v_recip(st[:, 0:2], v1[:, :, 128], [bb1], [stb])
                                    v_recip(st[:, 2:4], v2[:, :, 128], [bb2], [stb])
                                    v_ts(st[:, 2:4], st[:, 2:4], neg_lam, None, ALU.mult, None, [stb, B_gl], [stb])
                                    v_tt(oe[:, 0], v1[:, :, 0:128], st[:, 0:2].unsqueeze(2).to_broadcast([128, 2, 128]), ALU.mult, [bb1, stb], [B_oe])
                                    v_tt(oe[:, 1], v2[:, :, 0:128], st[:, 2:4].unsqueeze(2).to_broadcast([128, 2, 128]), ALU.mult, [bb2, stb], [B_oe])
                                    v_tt(oe2[:, :, :], oe[:, 0], oe[:, 1], ALU.add, [B_oe], [B_oe2])
                                    st2, st2b = rstd_groups(oe2[:, :, :], B_oe2, 2, 128, sq, B_sq)
                                    v_tt(oe[:, 0], oe2[:, :, :], st2[:, 0:2].unsqueeze(2).to_broadcast([128, 2, 128]), ALU.mult, [B_oe2, st2b], [B_oe])
                                    v_tt(ob16[:, :, :], oe[:, 0], gsl("dog2").unsqueeze(1).to_broadcast([128, 2, 128]), ALU.mult, [B_oe, B_gl], [B_ob16])
                                    tp, tb = t_half()
                                    tp3 = tp.rearrange("p (a b) -> p a b", a=4)
                                    for i in range(2):
                                        tr(tp3[:, i, :], ob16[:, i, :], [B_ob16], [tb])
                                    col = c * 512 + half * 256
                                    v_copy(oT_d[:, h, col:col + 256], tp[:, 0:256], [tb], [B_oTd[c]])

                    for g in range(2) if "B" in phases else ():
                        S.barrier()
                        kTm = ar.at(OFF_C, [128, 4, SEQ], BF16)
                        Vm = ar.at(OFF_D, [128, NT, 4, 65], BF16)
                        ar.reset(OFF_W)
                        hT = ar.alloc([128, 8, 128], BF16)
                        qT = ar.alloc([128, 4, 512], BF16)
                        cqT = ar.alloc([128, 3, 512], BF16)
                        ckvT = ar.alloc([128, 2, 512], BF16)
                        sqf = ar.alloc([128, 512], F32)
                        sq = sqf.bitcast(BF16)
                        hb_t = [ar.alloc([128, 1024], BF16) for _ in range(1)]
                        c16 = [ar.alloc([128, 384], BF16) for _ in range(1)]
                        kr = ar.alloc([128, 4, 32], F32)
                        krss = ar.alloc([128, 4], F32)
                        qn = ar.alloc([128, 4, 96], F32)
                        qr = ar.alloc([128, 4, 32], F32)
                        q16 = [ar.alloc([128, 4, 96], BF16) for _ in range(2)]
                        pT = [ar.alloc([128, 512], BF16) for _ in range(2)]
                        ostg = hb_t[0].rearrange("p (a b c) -> p a b c", a=4, b=4)
                        B_hT = [Buf(f"hT{t}") for t in range(4)]
                        B_qT = [Buf(f"qT{t}") for t in range(4)]
                        B_cqT = [Buf(f"cqT{t}") for t in range(4)]
                        B_ckvT = [Buf(f"ckvT{t}") for t in range(4)]
                        B_sq, B_hb = Buf("sq"), [Buf("hb0"), Buf("hb1")]
                        B_c16 = [Buf("c16a"), Buf("c16b")]
                        B_kr = [Buf(f"kr{t}") for t in range(4)]
                        B_qn, B_qr = Buf("qn"), Buf("qr")
                        B_q16 = [Buf("q16a"), Buf("q16b")]
                        B_pT = [Buf(f"pT{i}") for i in range(2)]
                        B_ostg = B_hb[0]
                        B_kT = [Buf(f"kTm{t}") for t in range(NT)]
                        B_V = [Buf(f"Vm{t}") for t in range(NT)]
                        v_memset(Vm[:, :, :, 64:65], 1.0, B_V)

                        def rope16(dst, src, cs_, sn_, G, rd, wr, tmpa, tmpb, B_tmp):
                            cb = cs_.unsqueeze(1).to_broadcast([128, G, 16])
                            snb = sn_.unsqueeze(1).to_broadcast([128, G, 16])
                            v_tt(tmpa[:, 0:G, 0:16], src[:, :, 0:16], cb, ALU.mult, rd + [B_rot], [B_tmp])
                            v_tt(tmpb[:, 0:G, 0:16], src[:, :, 16:32], snb, ALU.mult, rd + [B_rot], [B_tmp])
                            v_tt(tmpa[:, 0:G, 16:32], src[:, :, 16:32], cb, ALU.mult, rd + [B_rot], [B_tmp])
                            v_tt(tmpb[:, 0:G, 16:32], src[:, :, 0:16], snb, ALU.mult, rd + [B_rot], [B_tmp])
                            v_tt(dst[:, :, 0:16], tmpa[:, 0:G, 0:16], tmpb[:, 0:G, 0:16], ALU.subtract, [B_tmp], wr)
                            v_tt(dst[:, :, 16:32], tmpa[:, 0:G, 16:32], tmpb[:, 0:G, 16:32], ALU.add, [B_tmp], wr)

                        ra = ar.alloc([128, 4, 32], F32)
                        rb = ar.alloc([128, 4, 32], F32)
                        B_rab = Buf("rab")
                        for c in range(4):
                            ws, wb = wget("w_in", l, 0, 1024, C_CQ, 384)
                            ws2, wb2 = wget("w_in", l, 0, 1024, C_CKV, 288, hold=1)
                            for t in range(4):
                                make_hT(c, hT, [B_hT[0]] * 4, hb_t, B_hb, sq, B_sq, tiles=(t,), col0=-t * 128)
                                pb, pbb = proj_tile(hT, B_hT[0], t, ws, wb, 8, 384, col=0)
                                st, stb = stat()
                                act(sqf[:, 0:384], pb[:, 0:384], AF.Square, [pbb], [B_sq, stb], accum=st[:, 0:1])
                                rstd_from_ss(st, stb, 1, 384)
                                cb16, cbb = c16[0], B_c16[0]
                                v_stt(cb16[:, 0:384], pb[:, 0:384], st[:, 0:1], gsl("qag"), ALU.mult, ALU.mult, [pbb, stb, B_gl], [cbb])
                                tp, tb = t_half()
                                tp3 = tp.rearrange("p (a b) -> p a b", a=4)
                                for k in range(3):
                                    tr(tp3[:, k, :], cb16[:, k * 128:(k + 1) * 128], [cbb], [tb])
                                v_copy(cqT[:, :, t * 128:(t + 1) * 128], tp3[:, 0:3, :], [tb], [B_cqT[t]])
                                gt = 4 * c + t
                                pb, pbb = proj_tile(hT, B_hT[0], t, ws2, wb2, 8, 288, col=0)
                                st, stb = stat()
                                act(sqf[:, 0:256], pb[:, 0:256], AF.Square, [pbb], [B_sq, stb], accum=st[:, 0:1])
                                rstd_from_ss(st, stb, 1, 256)
                                cb16, cbb = c16[0], B_c16[0]
                                v_stt(cb16[:, 0:256], pb[:, 0:256], st[:, 0:1], gsl("kvag"), ALU.mult, ALU.mult, [pbb, stb, B_gl], [cbb])
                                tp, tb = t_half()
                                tp3 = tp.rearrange("p (a b) -> p a b", a=4)
                                for k in range(2):
                                    tr(tp3[:, k, :], cb16[:, k * 128:(k + 1) * 128], [cbb], [tb])
                                v_copy(ckvT[:, :, t * 128:(t + 1) * 128], tp3[:, 0:2, :], [tb], [B_ckvT[t]])
                                rope16(kr[:, t:t + 1, :], pb[:, 256:288].unsqueeze(1), cosr[:, gt, :], sinr[:, gt, :], 1, [pbb], [B_kr[t]], ra, rb, B_rab)
                                st2, st2b = stat()
                                act(sqf[:, 0:32], kr[:, t, :], AF.Square, [B_kr[t]], [B_sq, B_kr[t]], accum=krss[:, t:t + 1])
                            ws, wb = wget("w_mla_qb", l, 0, 384, 384 * g, 384)
                            for t in range(4):
                                gt = 4 * c + t
                                pb, pbb = s_bank()
                                for kc in range(3):
                                    mm(pb[:, 0:384], cqT[:, kc, t * 128:(t + 1) * 128], ws[:, kc, 0:384], kc == 0, kc == 2, [B_cqT[t], wb], [pbb])
                                p3 = pb[:, 0:384].rearrange("p (a b) -> p a b", a=4)
                                st, stb = rstd_groups(p3, pbb, 4, 96, sq, B_sq)
                                v_tt(qn[:, :, :], p3, st[:, 0:4].unsqueeze(2).to_broadcast([128, 4, 96]), ALU.mult, [pbb, stb], [B_qn])
                                rope16(qr[:, :, :], qn[:, :, 64:96], cosr[:, gt, :], sinr[:, gt, :], 4, [B_qn], [B_qr], ra, rb, B_rab)
                                v_copy(qn[:, :, 64:96], qr[:, :, :], [B_qr], [B_qn])
                                qq, qqb = q16[t % 2], B_q16[t % 2]
                                v_tt(qq[:, :, :], qn[:, :, :], gsl("mqk").unsqueeze(1).to_broadcast([128, 4, 96]), ALU.mult, [B_qn, B_gl], [qqb])
                                tp, tb = t_half()
                                tp3 = tp.rearrange("p (a b) -> p a b", a=4)
                                for hh in range(4):
                                    tr(tp3[0:96, hh, :], qq[:, hh, :], [qqb], [tb])
                                v_copy(qT[0:96, :, t * 128:(t + 1) * 128], tp3[0:96], [tb], [B_qT[t]])
                            ws, wb = wget("w_mla_kvb", l, 0, 256, 512 * g, 512)
                            for t in range(4):
                                gt = 4 * c + t
                                pb, pbb = s_bank()
                                for kc in range(2):
                                    mm(pb[:, :], ckvT[:, kc, t * 128:(t + 1) * 128], ws[:, kc, 0:512], kc == 0, kc == 1, [B_ckvT[t], wb], [pbb])
                                p3 = pb[:, :].rearrange("p (a b) -> p a b", a=4)
                                act(Vm[:, gt, :, 0:64], p3[:, :, 64:128], AF.Copy, [pbb], [B_V[gt]])
                                st, stb = stat()
                                sq3 = sqf[:, 0:256].rearrange("p (a b) -> p a b", a=4)
                                act(sq3, p3[:, :, 0:64], AF.Square, [pbb], [B_sq])
                                v_red(st[:, 0:4], sq3, [B_sq], [stb])
                                v_tt(st[:, 0:4], st[:, 0:4], krss[:, t:t + 1].to_broadcast([128, 4]), ALU.add, [stb, B_kr[t]], [stb])
                                rstd_from_ss(st, stb, 4, 96)
                                v_tt(qn[:, :, 0:64], p3[:, :, 0:64], st[:, 0:4].unsqueeze(2).to_broadcast([128, 4, 64]), ALU.mult, [pbb, stb], [B_qn])
                                v_tt(qn[:, :, 64:96], kr[:, t:t + 1, :].to_broadcast([128, 4, 32]), st[:, 0:4].unsqueeze(2).to_broadcast([128, 4, 32]), ALU.mult,
                                     [B_kr[t], stb], [B_qn])
                                qq, qqb = q16[t % 2], B_q16[t % 2]
                                v_copy(qq[:, :, :], qn[:, :, :], [B_qn], [qqb])
                                tp, tb = t_half()
                                tp3 = tp.rearrange("p (a b) -> p a b", a=4)
                                for hh in range(4):
                                    tr(tp3[0:96, hh, :], qq[:, hh, :], [qqb], [tb])
                                v_copy(kTm[0:96, :, gt * 128:(gt + 1) * 128], tp3[0:96], [tb], [B_kT[gt]])
                            nk = 4 * c + 4
                            for hh in range(4):
                                bi = 3 + (hh % 2)
                                O_list = [(banks[bi][:, i * 128:i * 128 + 65], bk[bi], i == 0) for i in range(4)]
                                attend(c, nk, True,
                                       lambda j, hh=hh: kTm[0:96, hh, j * 128:(j + 1) * 128],
                                       lambda q0, hh=hh: qT[0:96, hh, q0:512],
                                       lambda j, hh=hh: Vm[:, j, hh, :],
                                       65, 96 ** -0.5, O_list, pT, B_pT, B_qT, B_kT[0:nk], B_V[0:nk])
                                o3 = banks[bi][:, :].rearrange("p (a b) -> p a b", a=4)
                                st, stb = stat()
                                v_recip(st[:, 0:4], o3[:, :, 64], [bk[bi]], [stb])
                                v_tt(ostg[:, :, hh, :], o3[:, :, 0:64], st[:, 0:4].unsqueeze(2).to_broadcast([128, 4, 64]), ALU.mult, [bk[bi], stb], [B_ostg])
                            for t in range(4):
                                tp, tb = t_half()
                                tp3 = tp.rearrange("p (a b) -> p a b", a=4)
                                for pair in range(2):
                                    tr(tp3[:, pair, :], ostg[:, t, 2 * pair:2 * pair + 2, :].rearrange("p a b -> p (a b)"), [B_ostg], [tb])
                                col = c * 512 + t * 128
                                v_copy(oT_m[:, 2 * g:2 * g + 2, col:col + 128], tp3[:, 0:2, :], [tb], [B_oTm[c]])

                    if "C" in phases:
                        S.barrier()
                        kTc = ar.at(OFF_D, [128, 4, 256], BF16)
                        Vc = ar.at(OFF_D + 2048, [128, 2, 4, 129], BF16)
                        ar.reset(OFF_W)
                        hT = ar.alloc([128, 8, 512], BF16)
                        qT = ar.alloc([128, 4, 512], BF16)
                        sqf = ar.alloc([128, 512], F32)
                        sq = sqf.bitcast(BF16)
                        hb_t = [ar.alloc([128, 1024], BF16) for _ in range(1)]
                        qn = hb_t[0].bitcast(F32)
                        q16 = [ar.alloc([128, 512], BF16) for _ in range(2)]
                        pT = [ar.alloc([128, 512], BF16) for _ in range(2)]
                        ob16 = ar.alloc([128, 2, 128], BF16)
                        memx = qT.rearrange("p a b -> p (a b)").bitcast(F32)
                        B_hT = [Buf(f"hT{t}") for t in range(4)]
                        B_qT = [Buf(f"qT{t}") for t in range(4)]
                        B_sq, B_hb = Buf("sq"), [Buf("hb0"), Buf("hb1")]
                        B_qn, B_q16 = B_hb[0], [Buf("q16a"), Buf("q16b")]
                        B_pT = [Buf(f"pT{i}") for i in range(2)]
                        B_ob16, B_memx = Buf("ob16"), Buf("memx")
                        B_kTc, B_Vc = Buf("kTc"), Buf("Vc")
                        v_memset(Vc[:, :, :, 128:129], 1.0, [B_Vc])
                        load_gbig("mem_gain", l)
                        B_mT = [Buf("mT0"), Buf("mT1")]
                        for mt in range(2):
                            S.dma("sp", "mem", (lambda mt, b, memx: lambda e: e.dma_start(out=memx, in_=dd["mem"][b, mt * 128:(mt + 1) * 128, :]))(mt, b, memx), (), [B_memx])
                            make_hT(0, hT, [B_mT[mt]] * 4, hb_t, B_hb, sq, B_sq, src_tiles=[(memx, B_memx)], tiles=(0,), col0=mt * 128)
                        mcols = {0: 0, 1: 128}
                        for blk in range(2):
                            ws, wb = wget("w_mem_kv", l, 0, 1024, 512 * blk, 512)
                            for mt in range(2):
                                pb, pbb = s_bank()
                                c0 = mcols[mt]
                                for kc in range(8):
                                    mm(pb[:, :], hT[:, kc, c0:c0 + 128], ws[:, kc, 0:512], kc == 0, kc == 7, [B_mT[mt], wb], [pbb])
                                p3 = pb[:, :].rearrange("p (a b) -> p a b", a=4)
                                if blk == 0:
                                    st, stb = rstd_groups(p3, pbb, 4, 128, sq, B_sq)
                                    qq, qqb = q16[mt % 2], B_q16[mt % 2]
                                    v_tt(qq[:, :].rearrange("p (a b) -> p a b", a=4), p3, st[:, 0:4].unsqueeze(2).to_broadcast([128, 4, 128]), ALU.mult, [pbb, stb], [qqb])
                                    tp, tb = t_half()
                                    tp3 = tp.rearrange("p (a b) -> p a b", a=4)
                                    for hh in range(4):
                                        tr(tp3[:, hh, :], qq[:, hh * 128:(hh + 1) * 128], [qqb], [tb])
                                    v_copy(kTc[:, :, mt * 128:(mt + 1) * 128], tp3, [tb], [B_kTc])
                                else:
                                    act(Vc[:, mt, :, 0:128], p3, AF.Copy, [pbb], [B_Vc])
                        load_gbig("g_mix", l)
                        for c in range(4):
                            make_hT(c, hT, B_hT, hb_t, B_hb, sq, B_sq)
                            ws, wb = wget("w_in", l, 0, 1024, C_XQ, 512)
                            for t in range(4):
                                pb, pbb = proj_tile(hT, B_hT[t], t, ws, wb, 8, 512)
                                p3 = pb[:, :].rearrange("p (a b) -> p a b", a=4)
                                st, stb = rstd_groups(p3, pbb, 4, 128, sq, B_sq)
                                qn3 = qn[:, :].rearrange("p (a b) -> p a b", a=4)
                                v_tt(qn3, p3, st[:, 0:4].unsqueeze(2).to_broadcast([128, 4, 128]), ALU.mult, [pbb, stb], [B_qn])
                                qq, qqb = q16[t % 2], B_q16[t % 2]
                                v_tt(qq[:, :].rearrange("p (a b) -> p a b", a=4), qn3, gsl("cqk").unsqueeze(1).to_broadcast([128, 4, 128]), ALU.mult, [B_qn, B_gl], [qqb])
                                tp, tb = t_half()
                                tp3 = tp.rearrange("p (a b) -> p a b", a=4)
                                for hh in range(4):
                                    tr(tp3[:, hh, :], qq[:, hh * 128:(hh + 1) * 128], [qqb], [tb])
                                v_copy(qT[:, :, t * 128:(t + 1) * 128], tp3, [tb], [B_qT[t], B_memx])
                            for h in range(4):
                                bsel = 3 + 2 * (h % 2)
                                O_list = [(banks[bsel + i // 2][:, (i % 2) * 256:(i % 2) * 256 + 129], bk[bsel + i // 2], i % 2 == 0) for i in range(4)]
                                attend(c, 2, False,
                                       lambda j, h=h: kTc[:, h, j * 128:(j + 1) * 128],
                                       lambda q0, h=h: qT[:, h, q0:512],
                                       lambda j, h=h: Vc[:, j, h, :],
                                       129, 128 ** -0.5, O_list, pT, B_pT, B_qT, [B_kTc], [B_Vc])
                                for half in range(2):
                                    bi = bsel + half
                                    v1 = banks[bi][:, :].rearrange("p (a b) -> p a b", a=2)
                                    st, stb = stat()
                                    v_recip(st[:, 0:2], v1[:, :, 128], [bk[bi]], [stb])
                                    v_tt(ob16[:, :, :], v1[:, :, 0:128], st[:, 0:2].unsqueeze(2).to_broadcast([128, 2, 128]), ALU.mult, [bk[bi], stb], [B_ob16])
                                    tp, tb = t_half()
                                    tp3 = tp.rearrange("p (a b) -> p a b", a=4)
                                    for i in range(2):
                                        tr(tp3[:, i, :], ob16[:, i, :], [B_ob16], [tb])
                                    col = c * 512 + half * 256
                                    v_copy(oT_c[:, h, col:col + 256], tp[:, 0:256], [tb], [B_oTc[c]])

                    if "D" in phases:
                        S.barrier()
                        ar.reset(OFF_D)
                        hT = ar.alloc([128, 8, 512], BF16)
                        mT = ar.alloc([128, 8, 512], BF16)
                        sqf = ar.alloc([128, 512], F32)
                        sq = sqf.bitcast(BF16)
                        hb_t = [ar.alloc([128, 1024], BF16) for _ in range(1)]
                        mg = ar.alloc([128, 4, 512], F32)
                        gsig = [ar.alloc([128, 512], F32) for _ in range(1)]
                        tmp = [hb_t[0].bitcast(F32)]
                        m16 = [ar.alloc([128, 512], BF16) for _ in range(1)]
                        B_hT = [Buf(f"hT{t}") for t in range(4)]
                        B_mT = [Buf(f"mT{t}") for t in range(4)]
                        B_sq, B_hb = Buf("sq"), [Buf("hb0"), Buf("hb1")]
                        B_mg = [Buf(f"mg{t}") for t in range(4)]
                        B_gsig, B_tmp, B_m16 = [Buf("gs0"), Buf("gs1")], [B_hb[0]], [Buf("m16a"), Buf("m16b")]
                        branches = [("w_diff_o", oT_d, B_oTd, 8), ("w_mla_o", oT_m, B_oTm, 4), ("w_cross_o", oT_c, B_oTc, 4)]
                        cnt = 0
                        for c in range(4):
                            make_hT(c, hT, B_hT, hb_t, B_hb, sq, B_sq)
                            for nb in range(2):
                                for bi_, (wn, oT, B_oT, kcn) in enumerate(branches):
                                    wy, wyb = wget(wn, l, 0, kcn * 128, 512 * nb, 512)
                                    gcol = C_GATE + bi_ * 1024 + 512 * nb
                                    wg, wgb = wget("w_in", l, 0, 1024, gcol, 512, bias=(bi_ * 1024 + 512 * nb,), hold=1)
                                    for t in range(4):
                                        yb, ybb = s_bank()
                                        col = c * 512 + t * 128
                                        for kc in range(kcn):
                                            mm(yb[:, :], oT[:, kc, col:col + 128], wy[:, kc, 0:512], kc == 0, kc == kcn - 1, [B_oT[c], wyb], [ybb])
                                        lb, lbb = proj_tile(hT, B_hT[t], t, wg, wgb, 8, 512, extra_bias=True)
                                        gs, gsb = gsig[0], B_gsig[0]
                                        act(gs[:, :], lb[:, :], AF.Sigmoid, [lbb], [gsb])
                                        if bi_ == 0:
                                            v_tt(mg[:, t, :], gs[:, :], yb[:, :], ALU.mult, [gsb, ybb], [B_mg[t]])
                                        else:
                                            tm, tmb = tmp[0], B_tmp[0]
                                            v_tt(tm[:, :], gs[:, :], yb[:, :], ALU.mult, [gsb, ybb], [tmb])
                                            v_tt(mg[:, t, :], mg[:, t, :], tm[:, :], ALU.add, [B_mg[t], tmb], [B_mg[t]], eng="pool")
                                        cnt += 1
                                for t in range(4):
                                    mm16, mmb = m16[0], B_m16[0]
                                    v_copy(mm16[:, :], mg[:, t, :], [B_mg[t]], [mmb])
                                    tp, tb = t_half()
                                    tp3 = tp.rearrange("p (a b) -> p a b", a=4)
                                    for k in range(4):
                                        tr(tp3[:, k, :], mm16[:, k * 128:(k + 1) * 128], [mmb], [tb])
                                    v_copy(mT[:, nb * 4:(nb + 1) * 4, t * 128:(t + 1) * 128], tp3, [tb], [B_mT[t]])
                            for nb in range(2):
                                ws, wb = wget("w_out", l, 0, 1024, 512 * nb, 512)
                                for t in range(4):
                                    gt = 4 * c + t
                                    pb, pbb = proj_tile(mT, B_mT[t], t, ws, wb, 8, 512)
                                    xv = xs[:, gt, nb * 512:(nb + 1) * 512]
                                    v_tt(xv, xv, pb[:, :], ALU.add, [B_xs[gt], pbb], [B_xs[gt]])

                    if "E" in phases:
                        S.barrier()
                        load_gbig("g_mlp", l)
                        rT = ar.at(OFF_A, [128, 32, 512], BF16)
                        ar.reset(OFF_B)
                        hT = ar.alloc([128, 8, 512], BF16)
                        sqf = ar.alloc([128, 512], F32)
                        sq = sqf.bitcast(BF16)
                        hb_t = [ar.alloc([128, 1024], BF16) for _ in range(2)]
                        B_hT = [Buf(f"hT{t}") for t in range(4)]
                        B_sq, B_hb = Buf("sq"), [Buf("hb0"), Buf("hb1")]
                        B_rT = [Buf(f"rT{i}") for i in range(8)]
                        relu_t = [ar.alloc([128, 512], F32) for _ in range(2)]
                        B_relu = [Buf("relu0"), Buf("relu1")]
                        for c in range(4):
                            make_hT(c, hT, B_hT, hb_t, B_hb, sq, B_sq)
                            for fg in range(8):
                                ws, wb = wget("w_mlp1", l, 0, 1024, 512 * fg, 512)
                                for sbk in range(4):
                                    pb, pbb = s_bank()
                                    for kc in range(8):
                                        mm(pb[:, :], ws[:, kc, sbk * 128:(sbk + 1) * 128], hT[:, kc, :], kc == 0, kc == 7, B_hT + [wb], [pbb])
                                    rl, rlb = relu_t[(fg * 4 + sbk) % 2], B_relu[(fg * 4 + sbk) % 2]
                                    act(rl[:, :], pb[:, :], AF.Relu, [pbb], [rlb])
                                    v_tt(rT[:, fg * 4 + sbk, :], rl[:, :], rl[:, :], ALU.mult, [rlb], [B_rT[fg]], eng="pool")
                            for nb in range(2):
                                accs = [(banks[3 + t], bk[3 + t]) for t in range(4)]
                                for fg in range(4):
                                    ws, wb = wget("w_mlp2", l, fg * 1024, 1024, 512 * nb, 512)
                                    for t in range(4):
                                        for k in range(8):
                                            fc = fg * 8 + k
                                            mm(accs[t][0][:, :], rT[:, fc, t * 128:(t + 1) * 128], ws[:, k, 0:512], fc == 0, fc == 31,
                                               [B_rT[fc // 4], wb], [accs[t][1]])
                                for t in range(4):
                                    gt = 4 * c + t
                                    xv = xs[:, gt, nb * 512:(nb + 1) * 512]
                                    v_tt(xv, xv, accs[t][0][:, :], ALU.add, [B_xs[gt], accs[t][1]], [B_xs[gt]])
                    S.barrier()

                for c in range(4):
                    for t in range(4):
                        gt = 4 * c + t
                        S.dma("sp", f"xs{c}", (lambda gt, b: lambda e: e.dma_start(out=out_d[b, gt * 128:(gt + 1) * 128, :], in_=xs[:, gt, :]))(gt, b),
                              [B_xs[gt]], (), batch_left=4 - t)
            S.wait_all_dma("sp", [f"xs{c}" for c in range(4)])
            return wrec

        S0 = Sched()
        wplan = program(S0, None)
        S1 = Sched()
        program(S1, wplan)
        S1.prepare()
        S1.check()
        with nc.Block() as block:
            def mk(name):
                def f(eo):
                    S1.emit_one(name, eo, sems, dsems)
                return f
            block.tensor(mk("pe"))
            block.scalar(mk("act"))
            block.vector(mk("dve"))
            block.gpsimd(mk("pool"))
            block.sync(mk("sp"))
        stats_ = {e: len(S1.ops[e]) for e in ENGS}
    return nc, stats_


def make_consts():
    c = np.zeros((128, 512), np.float32)
    c[:, 0:128] = np.eye(128, dtype=np.float32)
    k = np.arange(128)[:, None]
    q = np.arange(128)[None, :]
    c[:, 128:256] = np.where(q >= k, 0.0, NEG).astype(np.float32)
    c[:, 256:384] = 1.0
    for o, dim in ((384, 64), (416, 32)):
        e = -(np.arange(0, dim, 2, dtype=np.float32)) / np.float32(dim)
        c[:, o:o + dim // 2] = np.power(np.float32(10000.0), e).astype(np.float32)[None, :]
    return c


_CACHE = {}


def kernel(**inputs):
    n_cores = 8
    B = inputs["x"].shape[0]
    per = B // n_cores
    key = (per, DEPTH)
    if key not in _CACHE:
        _CACHE[key] = build_program(per, DEPTH)[0]
    nc = _CACHE[key]
    consts = make_consts()
    in_maps = []
    for i in range(n_cores):
        m = {"x": np.ascontiguousarray(inputs["x"][i * per:(i + 1) * per]),
             "mem": np.ascontiguousarray(inputs["mem"][i * per:(i + 1) * per]),
             "pos": np.ascontiguousarray(np.asarray(inputs["positions"][i * per:(i + 1) * per]).astype(np.int32).reshape(per, NT, 128).transpose(0, 2, 1)),
             "consts": consts}
        for n, _ in WEIGHT_SPECS + VEC_SPECS:
            m[n] = np.ascontiguousarray(np.asarray(inputs[n], dtype=np.float32))
        in_maps.append(m)
    res = run_bass_kernel_spmd(nc, in_maps, core_ids=list(range(n_cores)))
    return np.concatenate([r["out"] for r in res.results], axis=0)
```
